# Optimizing a Trainium2 kernel written in Bass

```python
import math
import jax, jax.numpy as jnp
from jax import lax
import numpy as np

D_MODEL = 1024
BATCH = 4
SEQ = 4096
DEPTH = 2

N_MIXERS = 2
N_MAMBA_LAYERS = (DEPTH + 1) // 2
N_ATTN_LAYERS = DEPTH // 2

SSM_EXPAND = 2
D_INNER = SSM_EXPAND * D_MODEL
SSM_HEAD_DIM = 64
SSM_HEADS = D_INNER // SSM_HEAD_DIM
SSM_GROUPS = 4
SSM_HEADS_PER_GROUP = SSM_HEADS // SSM_GROUPS
SSM_STATE = 128
CONV_WIDTH = 4
SSD_CHUNK = 128
CONV_DIM = D_INNER + 2 * SSM_GROUPS * SSM_STATE
IN_PROJ_DIM = 2 * D_INNER + 2 * SSM_GROUPS * SSM_STATE + SSM_HEADS

ATT_HEAD_DIM = 64
ATT_HEADS = D_MODEL // ATT_HEAD_DIM
DIL_PATTERNS = ((128, 1), (512, 4), (2048, 16))
N_DIL_GROUPS = len(DIL_PATTERNS)
QKV_DIM = N_DIL_GROUPS * 3 * ATT_HEADS * ATT_HEAD_DIM

FFN_HIDDEN = ((-(-8 * D_MODEL // 3) + 255) // 256) * 256

PLE_DIM = 256

NORM_EPS = 1e-6

kernel_name = "hybrid_ssd_dilated_attn_trunk"


def rmsnorm(x, gain):
    xf = x.astype(jnp.float32)
    y = xf * lax.rsqrt(jnp.mean(xf * xf, axis=-1, keepdims=True) + NORM_EPS)
    return (y * gain.astype(jnp.float32)).astype(x.dtype)


def causal_depthwise_conv(u, w, bias):
    k_width, chans = w.shape
    out = lax.conv_general_dilated(
        u, w[:, None, :].astype(u.dtype), window_strides=(1,),
        padding=[(k_width - 1, 0)], dimension_numbers=("NWC", "WIO", "NWC"),
        feature_group_count=chans)
    return out + bias.astype(u.dtype)


def ssd_chunked(x, dt, a, bm, cm):
    f32 = jnp.float32
    b, t = x.shape[:2]
    nc, cl = t // SSD_CHUNK, SSD_CHUNK
    g, hg = SSM_GROUPS, SSM_HEADS_PER_GROUP
    xs = (x.astype(f32) * dt[..., None]).reshape(b, nc, cl, g, hg, SSM_HEAD_DIM)
    a_dt = (dt * a).reshape(b, nc, cl, g, hg).transpose(0, 1, 3, 4, 2)
    a_cs = jnp.cumsum(a_dt, axis=-1)
    bc = bm.astype(f32).reshape(b, nc, cl, g, SSM_STATE)
    cc = cm.astype(f32).reshape(b, nc, cl, g, SSM_STATE)
    causal = jnp.tril(jnp.ones((cl, cl), dtype=bool))
    seg = a_cs[..., :, None] - a_cs[..., None, :]
    lmat = jnp.exp(jnp.where(causal, seg, -jnp.inf))
    cb = jnp.einsum("bclgn,bcsgn->bcgls", cc, bc)
    y_diag = jnp.einsum("bcgls,bcghls,bcsghp->bclghp", cb, lmat, xs)
    decay = jnp.exp(a_cs[..., -1:] - a_cs)
    states = jnp.einsum("bclgn,bcghl,bclghp->bcghpn", bc, decay, xs)
    chunk_decay = jnp.exp(a_cs[..., -1])

    def step(carry, inp):
        st, dec = inp
        return carry * dec[..., None, None] + st, carry

    init = jnp.zeros((b, g, hg, SSM_HEAD_DIM, SSM_STATE), f32)
    _, prev = lax.scan(step, init, (jnp.moveaxis(states, 1, 0), jnp.moveaxis(chunk_decay, 1, 0)))
    prev = jnp.moveaxis(prev, 0, 1)
    y_off = jnp.einsum("bclgn,bcghpn,bcghl->bclghp", cc, prev, jnp.exp(a_cs))
    return (y_diag + y_off).reshape(b, t, SSM_HEADS, SSM_HEAD_DIM)


def mamba2_mixer(h, w_in, conv_w, conv_b, dt_bias, a_log, d_skip, norm_w, w_out):
    b, t, _ = h.shape
    zxbcdt = h @ w_in.astype(h.dtype)
    z = zxbcdt[..., :D_INNER]
    xbc = zxbcdt[..., D_INNER:D_INNER + CONV_DIM]
    dt_raw = zxbcdt[..., D_INNER + CONV_DIM:]
    xbc = jax.nn.silu(causal_depthwise_conv(xbc, conv_w, conv_b))
    xs = xbc[..., :D_INNER]
    bm = xbc[..., D_INNER:D_INNER + SSM_GROUPS * SSM_STATE].reshape(b, t, SSM_GROUPS, SSM_STATE)
    cm = xbc[..., D_INNER + SSM_GROUPS * SSM_STATE:].reshape(b, t, SSM_GROUPS, SSM_STATE)
    dt = jax.nn.softplus(dt_raw.astype(jnp.float32) + dt_bias.astype(jnp.float32))
    a = -jnp.exp(a_log.astype(jnp.float32))
    xh = xs.reshape(b, t, SSM_HEADS, SSM_HEAD_DIM)
    y = ssd_chunked(xh, dt, a, bm, cm)
    y = y + xh.astype(jnp.float32) * d_skip.astype(jnp.float32)[:, None]
    y = y.reshape(b, t, D_INNER) * jax.nn.silu(z.astype(jnp.float32))
    y = rmsnorm(y.reshape(b, t, SSM_GROUPS, -1), norm_w.reshape(SSM_GROUPS, -1)).reshape(b, t, D_INNER)
    return y.astype(h.dtype) @ w_out.astype(h.dtype)


def alibi_slopes(n_heads):
    return 2.0 ** (-8.0 * jnp.arange(1, n_heads + 1, dtype=jnp.float32) / n_heads)


def dilated_group_attention(q, k, v, window, dilation, slopes):
    f32 = jnp.float32
    b, t, nh, e = q.shape
    span = window // dilation
    blk = span
    lu = t // dilation
    nb = -(-lu // blk)
    lp = nb * blk

    def to_blocks(arr):
        arr = arr.reshape(b, lu, dilation, nh, e)
        arr = jnp.pad(arr, ((0, 0), (0, lp - lu), (0, 0), (0, 0), (0, 0)))
        return arr.reshape(b, nb, blk, dilation, nh, e)

    qb, kb, vb = to_blocks(q.astype(f32)), to_blocks(k.astype(f32)), to_blocks(v.astype(f32))
    pad_prev = ((0, 0), (1, 0), (0, 0), (0, 0), (0, 0), (0, 0))
    kcat = jnp.concatenate([jnp.pad(kb, pad_prev)[:, :nb], kb], axis=2)
    vcat = jnp.concatenate([jnp.pad(vb, pad_prev)[:, :nb], vb], axis=2)
    scores = jnp.einsum("bnqrhe,bnkrhe->bnrhqk", qb, kcat) * (1.0 / math.sqrt(e))
    qi = jnp.arange(blk)[:, None]
    ki = jnp.arange(2 * blk)[None, :]
    dist = qi + blk - ki
    in_band = (dist >= 0) & (dist <= span)
    key_u = jnp.arange(nb)[:, None] * blk - blk + jnp.arange(2 * blk)[None, :]
    valid = in_band[None] & (key_u >= 0)[:, None, :]
    bias = -slopes[:, None, None] * (dilation * dist).astype(f32)[None]
    logits = jnp.where(valid[None, :, None, None], scores + bias[None, None, None], -jnp.inf)
    lse = jax.nn.logsumexp(logits, axis=-1)
    probs = jnp.exp(logits - lse[..., None])
    out = jnp.einsum("bnrhqk,bnkrhe->bnqrhe", probs, vcat)
    out = out.reshape(b, lp, dilation, nh, e)[:, :lu].reshape(b, t, nh, e)
    lse = lse.transpose(0, 1, 4, 2, 3).reshape(b, lp, dilation, nh)[:, :lu].reshape(b, t, nh)
    return out, lse


def dilated_attention_mixer(h, w_qkv, q_gain, k_gain, w_o):
    b, t, _ = h.shape
    qkv = (h @ w_qkv.astype(h.dtype)).reshape(b, t, N_DIL_GROUPS, 3, ATT_HEADS, ATT_HEAD_DIM)
    q = rmsnorm(qkv[:, :, :, 0], q_gain)
    k = rmsnorm(qkv[:, :, :, 1], k_gain)
    v = qkv[:, :, :, 2]
    slopes = alibi_slopes(ATT_HEADS)
    outs, lses = [], []
    for g, (window, dilation) in enumerate(DIL_PATTERNS):
        o_g, l_g = dilated_group_attention(q[:, :, g], k[:, :, g], v[:, :, g], window, dilation, slopes)
        outs.append(o_g)
        lses.append(l_g)
    alpha = jax.nn.softmax(jnp.stack(lses), axis=0)
    o = jnp.einsum("gbth,gbthe->bthe", alpha, jnp.stack(outs))
    return o.reshape(b, t, ATT_HEADS * ATT_HEAD_DIM).astype(h.dtype) @ w_o.astype(h.dtype)


def swiglu(h, w_gate, w_up, w_down):
    return (jax.nn.silu(h @ w_gate.astype(h.dtype)) * (h @ w_up.astype(h.dtype))) @ w_down.astype(h.dtype)


def setup_inputs(seed: int = 0) -> dict:
    key = jax.random.key(seed)
    ks = jax.random.split(key, 24)
    f32 = jnp.float32

    def nrm(k, shape, scale):
        return jax.random.normal(k, shape, f32) * scale

    nm, na = N_MAMBA_LAYERS, N_ATTN_LAYERS
    dt0 = jnp.exp(jax.random.uniform(ks[8], (nm, SSM_HEADS), f32, math.log(1e-3), math.log(1e-1)))
    return {
        "x": nrm(ks[0], (BATCH, SEQ, D_MODEL), 1.0),
        "p": nrm(ks[1], (DEPTH, BATCH, SEQ, PLE_DIM), 1.0),
        "norm_mix": 1.0 + nrm(ks[2], (DEPTH, D_MODEL), 0.02),
        "norm_ffn": 1.0 + nrm(ks[3], (DEPTH, D_MODEL), 0.02),
        "ssm_w_in": nrm(ks[4], (nm, D_MODEL, IN_PROJ_DIM), D_MODEL ** -0.5),
        "ssm_conv_w": nrm(ks[5], (nm, CONV_WIDTH, CONV_DIM), CONV_WIDTH ** -0.5),
        "ssm_conv_b": nrm(ks[6], (nm, CONV_DIM), 0.02),
        "ssm_dt_bias": dt0 + jnp.log(-jnp.expm1(-dt0)),
        "ssm_a_log": jnp.log(jax.random.uniform(ks[9], (nm, SSM_HEADS), f32, 1.0, 16.0)),
        "ssm_d_skip": 1.0 + nrm(ks[10], (nm, SSM_HEADS), 0.1),
        "ssm_norm_w": 1.0 + nrm(ks[11], (nm, D_INNER), 0.02),
        "ssm_w_out": nrm(ks[12], (nm, D_INNER, D_MODEL), D_INNER ** -0.5),
        "att_w_qkv": nrm(ks[13], (na, D_MODEL, QKV_DIM), D_MODEL ** -0.5),
        "att_q_norm": 1.0 + nrm(ks[14], (na, ATT_HEAD_DIM), 0.02),
        "att_k_norm": 1.0 + nrm(ks[15], (na, ATT_HEAD_DIM), 0.02),
        "att_w_o": nrm(ks[16], (na, ATT_HEADS * ATT_HEAD_DIM, D_MODEL), (ATT_HEADS * ATT_HEAD_DIM) ** -0.5),
        "ffn_w_gate": nrm(ks[17], (DEPTH, D_MODEL, FFN_HIDDEN), D_MODEL ** -0.5),
        "ffn_w_up": nrm(ks[18], (DEPTH, D_MODEL, FFN_HIDDEN), D_MODEL ** -0.5),
        "ffn_w_down": nrm(ks[19], (DEPTH, FFN_HIDDEN, D_MODEL), FFN_HIDDEN ** -0.5),
        "ple_w_proj": nrm(ks[20], (DEPTH, PLE_DIM, D_MODEL), PLE_DIM ** -0.5),
        "ple_w_gate": nrm(ks[21], (DEPTH, D_MODEL, D_MODEL), D_MODEL ** -0.5),
    }


def reference(x, p, norm_mix, norm_ffn, ssm_w_in, ssm_conv_w, ssm_conv_b, ssm_dt_bias,
              ssm_a_log, ssm_d_skip, ssm_norm_w, ssm_w_out, att_w_qkv, att_q_norm,
              att_k_norm, att_w_o, ffn_w_gate, ffn_w_up, ffn_w_down, ple_w_proj, ple_w_gate):
    for i in range(DEPTH):
        j = i // N_MIXERS
        h = rmsnorm(x, norm_mix[i])
        if i % N_MIXERS == 0:
            mix = mamba2_mixer(h, ssm_w_in[j], ssm_conv_w[j], ssm_conv_b[j], ssm_dt_bias[j],
                               ssm_a_log[j], ssm_d_skip[j], ssm_norm_w[j], ssm_w_out[j])
        else:
            mix = dilated_attention_mixer(h, att_w_qkv[j], att_q_norm[j], att_k_norm[j], att_w_o[j])
        x = x + mix.astype(x.dtype)
        x = x + swiglu(rmsnorm(x, norm_ffn[i]), ffn_w_gate[i], ffn_w_up[i], ffn_w_down[i]).astype(x.dtype)
        gate = jax.nn.sigmoid((x @ ple_w_gate[i].astype(x.dtype)).astype(jnp.float32))
        ple = (p[i].astype(x.dtype) @ ple_w_proj[i].astype(x.dtype)).astype(jnp.float32)
        x = x + (gate * ple).astype(x.dtype)
    return x
```

```python
import contextlib
import os
import math
import numpy as np
import ml_dtypes
import concourse.bass as bass
import concourse.mybir as mybir
from concourse.bass_utils import run_bass_kernel_spmd

F32 = mybir.dt.float32
BF16 = mybir.dt.bfloat16
ALU = mybir.AluOpType
AF = mybir.ActivationFunctionType

ENGS = ("pe", "act", "dve", "pool", "sp")
NDMASEM = 6
EPS = 1e-6

D = 1024
TOK = 2048
NCH = 16
FF = 2816
NFC = 22


class Op:
    __slots__ = ("eng", "fn", "deps", "is_dma", "signal", "count", "sem", "dmaprev")

    def __init__(self, eng, fn, is_dma):
        self.eng = eng
        self.fn = fn
        self.deps = []
        self.is_dma = is_dma
        self.signal = False
        self.count = None
        self.sem = None
        self.dmaprev = None


class Sched:
    def __init__(self, nc, same_engine_sync=True):
        self.nc = nc
        self.ops = {e: [] for e in ENGS}
        self.lastw = {}
        self.readers = {}
        self.same = same_engine_sync
        self.ndma = {e: 0 for e in ENGS}
        self.dma_ring = {e: [] for e in ENGS}
        self.barrier_deps = {e: [] for e in ENGS}

    def op(self, eng, fn, reads=(), writes=(), dma=False):
        o = Op(eng, fn, dma)
        deps = list(self.barrier_deps[eng])
        self.barrier_deps[eng] = []
        force = set(id(d) for d in deps)
        for t in reads:
            w = self.lastw.get(t)
            if w is not None:
                deps.append(w)
        for t in writes:
            w = self.lastw.get(t)
            if w is not None:
                deps.append(w)
            deps.extend(self.readers.get(t, ()))
        seen = set()
        for d in deps:
            if id(d) in seen:
                continue
            seen.add(id(d))
            if d.eng == eng and not d.is_dma and id(d) not in force:
                if not (self.same and eng in ("act", "dve", "pool")):
                    continue
            o.deps.append(d)
            d.signal = True
        self.ops[eng].append(o)
        if dma:
            ring = self.dma_ring[eng]
            o.sem = ("dma", eng, self.ndma[eng] % NDMASEM)
            o.count = 16 * (self.ndma[eng] // NDMASEM + 1)
            if len(ring) >= NDMASEM:
                o.dmaprev = ring[-NDMASEM]
            ring.append(o)
            self.ndma[eng] += 1
        for t in reads:
            self.readers.setdefault(t, []).append(o)
        for t in writes:
            self.lastw[t] = o
            self.readers[t] = []
        return o

    def barrier(self):
        lasts = []
        for e in ENGS:
            if self.ops[e]:
                for o in reversed(self.ops[e]):
                    if not o.is_dma:
                        lasts.append(o)
                        break
                lasts.extend(self.dma_ring[e][-NDMASEM:])
        for e in ENGS:
            self.barrier_deps[e] = list(lasts)
        self.lastw = {}
        self.readers = {}

    def emit(self, sems):
        nc = self.nc
        engobj = {"pe": nc.tensor, "act": nc.scalar, "dve": nc.vector, "pool": nc.gpsimd, "sp": nc.sync}
        for e in ENGS:
            c = 0
            for o in self.ops[e]:
                if o.is_dma:
                    continue
                if o.signal:
                    c += 1
                    o.count = c
                    o.sem = e
        for e in ENGS:
            eo = engobj[e]
            known = {}
            for o in self.ops[e]:
                need = {}
                deps = list(o.deps)
                if o.dmaprev is not None:
                    deps.append(o.dmaprev)
                for d in deps:
                    if d.count is None:
                        continue
                    if known.get(d.sem, 0) >= d.count:
                        continue
                    if need.get(d.sem, 0) < d.count:
                        need[d.sem] = d.count
                for s, c in need.items():
                    eo.wait_ge(sems[s], c)
                    known[s] = c
                ins = o.fn(eo)
                if ins is None:
                    continue
                if o.is_dma:
                    ins.then_inc(sems[o.sem], 16)
                elif o.signal:
                    ins.then_inc(sems[o.sem], 1)


class Ctx:
    def __init__(self, name="k"):
        self.nc = bass.Bass("TRN2", target_bir_lowering=False)
        self.S = Sched(self.nc, same_engine_sync=(os.environ.get("NOSAME") is None))
        self.es = contextlib.ExitStack()
        self.outs = []
        self.P = None

    def din(self, name, shape, dt=F32):
        return self.nc.dram_tensor(name, list(shape), dt, kind="ExternalInput").ap()

    def dout(self, name, shape, dt=F32):
        return self.nc.dram_tensor(name, list(shape), dt, kind="ExternalOutput").ap()

    def sb(self, name, shape, dt, es=None):
        es = es or self.es
        t = es.enter_context(self.nc.sbuf_tensor(name, list(shape), dt))
        nb = int(np.prod(shape[1:])) * (2 if dt == BF16 else 4)
        pad = (-nb) % 64
        if pad:
            es.enter_context(self.nc.sbuf_tensor(name + "_pad", [128, pad // 2], BF16))
        return t

    def ps(self, name, shape, dt):
        return self.es.enter_context(self.nc.psum_tensor(name, list(shape), dt))

    def finish(self):
        S = self.S
        S.op("sp", lambda e: None, reads=list(self.outs))
        sems = {}
        for e in ENGS:
            sems[e] = self.es.enter_context(self.nc.semaphore("s_" + e))
        for e in ("sp", "pool", "act"):
            for i in range(NDMASEM):
                sems[("dma", e, i)] = self.es.enter_context(self.nc.semaphore(f"d_{e}{i}"))
        S.emit(sems)
        self.es.close()
        return self.nc


def bc(ap, shape):
    return ap.to_broadcast(list(shape))


def load_consts(K, names):
    S = K.S
    out = {}
    for n, (shape, dt) in names.items():
        d = K.din(n, shape, dt)
        t = K.sb("c_" + n, shape, dt)
        S.op("sp", lambda e, t=t, d=d: e.dma_start(out=t[:], in_=d), writes=[("c", n)], dma=True)
        out[n] = t
    return out


def load_bcast(K, name, width, es=None):
    d = K.din(name, [1, width], F32)
    t = K.sb("b_" + name, [128, width], F32, es)
    K.S.op("sp", lambda e: e.dma_start(out=t[:], in_=d.partition_broadcast(128)), writes=[("b", name)], dma=True)
    return t


SS_TAGS = {"pre": 0, "mix": 1, "ffn": 2, "att": 3, "attp": 4, "ffn2": 5}


def norm_to_hT(K, C, src_fn, src_tok_fn, gain, gain_tok, hT, tag, pre=None):
    S = K.S
    wk = K.wk
    for c in range(NCH):
        if pre is not None:
            pre(c)
        src = src_fn(c)
        st = src_tok_fn(c)
        ss = wk["ss"][:, SS_TAGS[tag], c:c + 1]
        S.op("act", lambda e, src=src, ss=ss: e.activation(out=wk["junk"][:], in_=src, func=AF.Square, scale=1.0 / 32.0, accum_out=ss),
             reads=[st], writes=["junk", ("ss", tag, c)])
        S.op("act", lambda e, ss=ss: e.activation(out=ss, in_=ss, func=AF.Sqrt, bias=C["eps"][:, 0:1], scale=1.0),
             reads=[("ss", tag, c), ("c", "eps")], writes=[("ss", tag, c)])
        S.op("dve", lambda e, ss=ss: e.reciprocal(out=ss, in_=ss), reads=[("ss", tag, c)], writes=[("ss", tag, c)])
        hb = wk["hb"][c % 2]
        S.op("dve", lambda e, src=src, ss=ss, hb=hb: e.scalar_tensor_tensor(out=hb[:], in0=src, scalar=ss, in1=gain[:], op0=ALU.mult, op1=ALU.mult),
             reads=[st, ("ss", tag, c), gain_tok], writes=[("hb", c % 2)])
        for k in range(8):
            S.op("pe", lambda e, hb=hb, k=k: e.transpose(out=K.pT[:, k * 128:(k + 1) * 128], in_=hb[:, k * 128:(k + 1) * 128], identity=C["ident"][:]),
                 reads=[("hb", c % 2), ("c", "ident")], writes=["pT"])
        S.op("act", lambda e, c=c: e.copy(out=hT[:, :, c * 128:(c + 1) * 128], in_=K.pT[:, :].rearrange("p (k t) -> p k t", k=8)),
             reads=["pT"], writes=[("hT", c)])


def ffn_block(K, C, x_sb, hT, gffn_name, wgate, wup, wdown):
    S = K.S
    with contextlib.ExitStack() as es:
        gain = load_bcast(K, gffn_name, D, es)
        wgs = [K.sb(f"wgs{i}", [128, 8, 512], BF16, es) for i in range(2)]
        wus = [K.sb(f"wus{i}", [128, 8, 512], BF16, es) for i in range(2)]
        wds = [K.sb(f"wds{i}", [128, 4, 1024], BF16, es) for i in range(2)]
        sgb = [K.sb(f"sgb{i}", [128, 512], F32, es) for i in range(2)]
        hid = [K.sb(f"hid{i}", [128, 4, 512], BF16, es) for i in range(2)]
        norm_to_hT(K, C, lambda c: x_sb[:, c, :], lambda c: ("x", c), gain, ("b", gffn_name), hT, "ffn")
        slices = [(f0, min(f0 + 4, NFC)) for f0 in range(0, NFC, 4)]
        P = K.P
        it = 0
        for si, (f0, f1) in enumerate(slices):
            nf = f1 - f0
            b = si % 2
            S.op("pool", lambda e, b=b, f0=f0, f1=f1, nf=nf: e.dma_start(out=wgs[b][:, :, 0:nf * 128], in_=wgate[:, f0 * 128:f1 * 128].rearrange("(kc p) c -> p kc c", p=128)),
                 writes=[("wgs", b)], dma=True)
            S.op("pool", lambda e, b=b, f0=f0, f1=f1, nf=nf: e.dma_start(out=wus[b][:, :, 0:nf * 128], in_=wup[:, f0 * 128:f1 * 128].rearrange("(kc p) c -> p kc c", p=128)),
                 writes=[("wus", b)], dma=True)
            S.op("pool", lambda e, b=b, f0=f0, f1=f1, nf=nf: e.dma_start(out=wds[b][:, 0:nf, :], in_=wdown[f0 * 128:f1 * 128, :].rearrange("(fc p) c -> p fc c", p=128)),
                 writes=[("wds", b)], dma=True)
            for tt in range(4):
                hb_ = hid[it % 2]
                htok = ("hid", it % 2)
                it += 1
                for fc in range(nf):
                    for k in range(8):
                        S.op("pe", lambda e, b=b, fc=fc, k=k, tt=tt: e.matmul(P[0][:, :], lhsT=wgs[b][:, k, fc * 128:(fc + 1) * 128], rhs=hT[:, k, tt * 512:(tt + 1) * 512], start=(k == 0), stop=(k == 7)),
                             reads=[("wgs", b)] + [("hT", tt * 4 + j) for j in range(4)], writes=[("P", 0)])
                    for k in range(8):
                        S.op("pe", lambda e, b=b, fc=fc, k=k, tt=tt: e.matmul(P[1][:, :], lhsT=wus[b][:, k, fc * 128:(fc + 1) * 128], rhs=hT[:, k, tt * 512:(tt + 1) * 512], start=(k == 0), stop=(k == 7)),
                             reads=[("wus", b)] + [("hT", tt * 4 + j) for j in range(4)], writes=[("P", 1)])
                    sg = sgb[fc % 2]
                    S.op("act", lambda e, sg=sg: e.activation(out=sg[:], in_=P[0][:, :], func=AF.Silu), reads=[("P", 0)], writes=[("sgb", fc % 2)])
                    S.op("dve", lambda e, sg=sg, fc=fc, hb_=hb_: e.tensor_tensor(out=hb_[:, fc, :], in0=sg[:], in1=P[1][:, :], op=ALU.mult),
                         reads=[("sgb", fc % 2), ("P", 1)], writes=[htok])
                for c in range(4):
                    cc = tt * 4 + c
                    for half in range(2):
                        pb = 2 + half
                        for fc in range(nf):
                            S.op("pe", lambda e, b=b, fc=fc, c=c, half=half, pb=pb, hb_=hb_, nf=nf: e.matmul(P[pb][:, :], lhsT=hb_[:, fc, c * 128:(c + 1) * 128], rhs=wds[b][:, fc, half * 512:(half + 1) * 512], start=(fc == 0), stop=(fc == nf - 1)),
                                 reads=[htok, ("wds", b)], writes=[("P", pb)])
                        S.op("dve", lambda e, cc=cc, half=half, pb=pb: e.tensor_tensor(out=x_sb[:, cc, half * 512:(half + 1) * 512], in0=x_sb[:, cc, half * 512:(half + 1) * 512], in1=P[pb][:, :], op=ALU.add),
                             reads=[("x", cc), ("P", pb)], writes=[("x", cc)])
        S.barrier()


def ple_block(K, C, x_sb, hT, p_dram, wpp, wpg):
    S = K.S
    P = K.P
    with contextlib.ExitStack() as es:
        wpg_sb = K.sb("wpg_sb", [128, 8, 1024], BF16, es)
        wpp_sb = K.sb("wpp_sb", [128, 2, 1024], BF16, es)
        ptm = K.sb("ptm", [128, NCH, 256], BF16, es)
        pTs = K.sb("pTs", [128, 2, TOK], BF16, es)
        sg = [K.sb(f"plsg{i}", [128, 512], F32, es) for i in range(2)]
        xb = [K.sb(f"plxb{i}", [128, 1024], BF16, es) for i in range(2)]
        S.op("pool", lambda e: e.dma_start(out=wpg_sb[:], in_=wpg.rearrange("(kc p) c -> p kc c", p=128)), writes=["wpg"], dma=True)
        S.op("pool", lambda e: e.dma_start(out=wpp_sb[:], in_=wpp.rearrange("(kc p) c -> p kc c", p=128)), writes=["wpp"], dma=True)
        S.op("pool", lambda e: e.dma_start(out=ptm[:], in_=p_dram.rearrange("(c p) f -> p c f", p=128)), writes=["ptm"], dma=True)
        for c in range(NCH):
            b = c % 2
            S.op("act", lambda e, c=c, b=b: e.copy(out=xb[b][:], in_=x_sb[:, c, :]), reads=[("x", c)], writes=[("plxb", b)])
            for k in range(8):
                S.op("pe", lambda e, b=b, k=k: e.transpose(out=K.pT[:, k * 128:(k + 1) * 128], in_=xb[b][:, k * 128:(k + 1) * 128], identity=C["ident"][:]),
                     reads=[("plxb", b), ("c", "ident")], writes=["pT"])
            S.op("act", lambda e, c=c: e.copy(out=hT[:, :, c * 128:(c + 1) * 128], in_=K.pT[:, :].rearrange("p (k t) -> p k t", k=8)),
                 reads=["pT"], writes=[("hT", c)])
            for k in range(2):
                S.op("pe", lambda e, c=c, k=k: e.transpose(out=K.pT[:, k * 128:(k + 1) * 128], in_=ptm[:, c, k * 128:(k + 1) * 128], identity=C["ident"][:]),
                     reads=["ptm", ("c", "ident")], writes=["pT"])
            S.op("act", lambda e, c=c: e.copy(out=pTs[:, :, c * 128:(c + 1) * 128], in_=K.pT[:, 0:256].rearrange("p (k t) -> p k t", k=2)),
                 reads=["pT"], writes=[("pTs", c)])
            for half in range(2):
                for k in range(8):
                    S.op("pe", lambda e, c=c, k=k, half=half: e.matmul(P[0][:, :], lhsT=hT[:, k, c * 128:(c + 1) * 128], rhs=wpg_sb[:, k, half * 512:(half + 1) * 512], start=(k == 0), stop=(k == 7)),
                         reads=[("hT", c), "wpg"], writes=[("P", 0)])
                for k in range(2):
                    S.op("pe", lambda e, c=c, k=k, half=half: e.matmul(P[1][:, :], lhsT=pTs[:, k, c * 128:(c + 1) * 128], rhs=wpp_sb[:, k, half * 512:(half + 1) * 512], start=(k == 0), stop=(k == 1)),
                         reads=[("pTs", c), "wpp"], writes=[("P", 1)])
                s_ = sg[half]
                S.op("act", lambda e, s_=s_: e.activation(out=s_[:], in_=P[0][:, :], func=AF.Sigmoid), reads=[("P", 0)], writes=[("plsg", half)])
                S.op("dve", lambda e, s_=s_: e.tensor_tensor(out=s_[:], in0=s_[:], in1=P[1][:, :], op=ALU.mult), reads=[("plsg", half), ("P", 1)], writes=[("plsg", half)])
                S.op("dve", lambda e, s_=s_, c=c, half=half: e.tensor_tensor(out=x_sb[:, c, half * 512:(half + 1) * 512], in0=x_sb[:, c, half * 512:(half + 1) * 512], in1=s_[:], op=ALU.add),
                     reads=[("plsg", half), ("x", c)], writes=[("x", c)])
        S.barrier()


def setup_common(K):
    K.P = [K.ps(f"P{i}", [128, 512], F32) for i in range(7)]
    K.pT = K.ps("pT", [128, 1024], BF16)
    C = load_consts(K, {"ident": ([128, 128], BF16), "U": ([128, 128], F32), "ones": ([128, 128], F32),
                        "sel": ([128, 8, 128], BF16), "negmask": ([128, 128], BF16)})
    eps = K.sb("c_eps", [128, 1], F32)
    one = K.sb("c_one", [128, 1], F32)
    K.S.op("dve", lambda e: e.memset(eps[:], EPS), writes=[("c", "eps")])
    K.S.op("dve", lambda e: e.memset(one[:], 1.0), writes=[("c", "one")])
    C["eps"] = eps
    C["one"] = one
    K.wk = {"ss": K.sb("ss", [128, 6, NCH], F32), "junk": K.sb("junk", [128, 1024], BF16),
            "hb": [K.sb(f"hb{i}", [128, 1024], BF16) for i in range(2)]}
    K.S.op("dve", lambda e: e.memset(K.wk["ss"][:], 0.0), writes=[("ss", t, c) for t in SS_TAGS for c in range(NCH)])
    return C


def mamba_mixer(K, C, x_sb, hT, xp_dram):
    S = K.S
    P = K.P
    pT = K.pT
    flag = K.din("flag", [128, 1], F32)
    wg = K.din("wg", [4, D, 1288], F32)
    wout = K.din("wout", [2048, D], F32)
    cw_d = K.din("cw", [128, 4, 6, 4], F32)
    cb_d = K.din("cb", [128, 4, 6], F32)
    with contextlib.ExitStack() as es:
        gain = load_bcast(K, "gmix", D, es)
        dtb = load_bcast(K, "dtb", 32, es)
        abc = load_bcast(K, "alog", 32, es)
        dsk = load_bcast(K, "dsk", 32, es)
        normw = load_bcast(K, "normw", 2048, es)
        flag_sb = K.sb("flag_sb", [128, 1], F32, es)
        cw = K.sb("cw_sb", [128, 4, 6, 4], F32, es)
        cb = K.sb("cb_sb", [128, 4, 6], F32, es)
        S.op("sp", lambda e: e.dma_start(out=flag_sb[:], in_=flag), writes=["flag"], dma=True)
        S.op("sp", lambda e: e.dma_start(out=cw[:], in_=cw_d), writes=["cw"], dma=True)
        S.op("sp", lambda e: e.dma_start(out=cb[:], in_=cb_d), writes=["cb"], dma=True)
        S.op("act", lambda e: e.activation(out=abc[:], in_=abc[:], func=AF.Exp), reads=[("b", "alog")], writes=[("b", "alog")])
        S.op("dve", lambda e: e.tensor_scalar(out=abc[:], in0=abc[:], scalar1=-1.0, scalar2=None, op0=ALU.mult), reads=[("b", "alog")], writes=[("b", "alog")])
        wg_sb = K.sb("wg_sb", [128, 8, 1288], BF16, es)
        wo_sb = K.sb("wo_sb", [128, 4, 1024], BF16, es)
        xst = [K.sb(f"xst{i}", [128, 1024], F32, es) for i in range(2)]
        ub = [K.sb(f"ub{i}", [128, 515], F32, es) for i in range(2)]
        acc = [K.sb(f"acc{i}", [128, 512], F32, es) for i in range(2)]
        xcT = K.sb("xcT", [128, 6, 512], BF16, es)
        hal = K.sb("hal", [128, 4, 6, 3], F32, es)
        S_f = K.sb("S_f", [128, 4, 512], F32, es)
        S_b = K.sb("S_b", [128, 4, 512], BF16, es)
        xbtm = K.sb("xbtm", [128, 640], BF16, es)
        sm = K.sb("sm", [128, 16, 8], F32, es)
        adt_t = K.sb("adt_t", [128, 8], F32, es)
        r1 = K.sb("rsplit_a", [128, 8], F32, es)
        r2 = K.sb("rsplit_b", [128, 8], F32, es)
        cs3tm = K.sb("cs3tm", [128, 128], BF16, es)
        Ff = K.sb("Ffull", [128, 128], F32, es)
        Hb = K.sb("Hbfull", [128, 128], BF16, es)
        Hf = K.sb("Hffull", [128, 128], F32, es)
        cs3 = K.sb("cs3", [128, 128], BF16, es)
        ncs3 = K.sb("ncs3", [128, 128], BF16, es)
        Lt = K.sb("Lt", [128, 8, 128], BF16, es)
        cbt = K.sb("cbt", [128, 128], BF16, es)
        MT = K.sb("MT", [128, 8, 128], BF16, es)
        xs = K.sb("xs", [128, 512], BF16, es)
        xsd = K.sb("xsd", [128, 512], BF16, es)
        zs = K.sb("zs", [128, 512], F32, es)
        yb = K.sb("yb", [128, 512], F32, es)
        tmp = K.sb("tmp", [128, 512], F32, es)
        tmp2 = K.sb("tmp2", [128, 512], F32, es)
        ssg = K.sb("ssg", [128, 1], F32, es)
        yn = K.sb("yn", [128, 512], BF16, es)
        ynT = K.sb("ynT", [128, 4, 128], BF16, es)

        S.op("dve", lambda e: e.memset(hal[:], 0.0), writes=["hal"])
        S.op("dve", lambda e: e.memset(S_f[:], 0.0), writes=[("S_f", g) for g in range(4)])
        S.op("dve", lambda e: e.memset(S_b[:], 0.0), writes=[("S_b", g) for g in range(4)])
        S.op("dve", lambda e: e.memset(cs3tm[:], 0.0), writes=["cs3tm"])
        S.op("dve", lambda e: e.memset(Ff[:], 0.0), writes=["Ff"])
        S.op("dve", lambda e: e.memset(ssg[:], 0.0), writes=["ssg"])

        def sm_(i):
            return sm[:, i, :]

        def h3(ap):
            return ap.rearrange("p (h e) -> p h e", h=8)

        def b3(ap):
            return ap.unsqueeze(2).to_broadcast([128, 8, 64])

        def tile_pass(g, tt, full):
            g8 = slice(g * 8, (g + 1) * 8)
            hT_t = [("hT", tt * 4 + j) for j in range(4)]
            for j in range(6):
                wc = 512 + j * 128
                pb = j % 2
                for k in range(8):
                    S.op("pe", lambda e, k=k, wc=wc, pb=pb: e.matmul(P[pb][:, :], lhsT=wg_sb[:, k, wc:wc + 128], rhs=hT[:, k, tt * 512:(tt + 1) * 512], start=(k == 0), stop=(k == 7)),
                         reads=["wg_sb"] + hT_t, writes=[("P", pb)])
                u = ub[pb]
                ut = ("ub", pb)
                S.op("act", lambda e, u=u, j=j: e.copy(out=u[:, 0:3], in_=hal[:, g, j, :]), reads=["hal"], writes=[ut])
                S.op("act", lambda e, u=u, pb=pb: e.copy(out=u[:, 3:515], in_=P[pb][:, :]), reads=[("P", pb)], writes=[ut])
                a_ = acc[pb]
                at = ("acc", pb)
                S.op("act", lambda e, u=u, a_=a_, j=j: e.activation(out=a_[:], in_=u[:, 3:515], func=AF.Identity, bias=cb[:, g, j:j + 1], scale=cw[:, g, j, 3:4]),
                     reads=[ut, "cw", "cb"], writes=[at])
                for kk in (2, 1, 0):
                    S.op("dve", lambda e, u=u, a_=a_, j=j, kk=kk: e.scalar_tensor_tensor(out=a_[:], in0=u[:, kk:kk + 512], scalar=cw[:, g, j, kk:kk + 1], in1=a_[:], op0=ALU.mult, op1=ALU.add),
                         reads=[ut, at, "cw"], writes=[at])
                S.op("act", lambda e, a_=a_, j=j: e.activation(out=xcT[:, j, :], in_=a_[:], func=AF.Silu), reads=[at], writes=[("xcT", j)])
                S.op("dve", lambda e, u=u, j=j: e.tensor_copy(out=hal[:, g, j, :], in_=u[:, 512:515]), reads=[ut], writes=["hal"])
            for c in range(4):
                cc = tt * 4 + c
                cs_ = slice(c * 128, (c + 1) * 128)
                gs_ = slice(cc * 128, (cc + 1) * 128)
                for j in range(5):
                    S.op("pe", lambda e, j=j, cs_=cs_: e.transpose(out=pT[:, j * 128:(j + 1) * 128], in_=xcT[:, j, cs_], identity=C["ident"][:]),
                         reads=[("xcT", j), ("c", "ident")], writes=["pT"])
                S.op("act", lambda e: e.copy(out=xbtm[:], in_=pT[:, 0:640]), reads=["pT"], writes=["xbtm"])
                for k in range(8):
                    S.op("pe", lambda e, k=k, gs_=gs_: e.matmul(P[4][:, 0:8], lhsT=hT[:, k, gs_], rhs=wg_sb[:, k, 1280:1288], start=(k == 0), stop=(k == 7)),
                         reads=["wg_sb", ("hT", cc)], writes=[("P", 4)])
                t_, ab_, dt_, adt_, cst_, dd_, wdt_, etot_, ecs_ = (sm_(i) for i in range(9))
                adt_ = adt_t[:, :]
                S.op("dve", lambda e: e.tensor_tensor(out=t_, in0=P[4][:, 0:8], in1=dtb[:, g8], op=ALU.add), reads=[("P", 4), ("b", "dtb")], writes=["sm_t"])
                S.op("dve", lambda e: e.scalar_tensor_tensor(out=ab_, in0=t_, scalar=-1.0, in1=t_, op0=ALU.mult, op1=ALU.max), reads=["sm_t"], writes=["sm_ab"])
                S.op("act", lambda e: e.activation(out=ab_, in_=ab_, func=AF.Exp, scale=-1.0), reads=["sm_ab"], writes=["sm_ab"])
                S.op("act", lambda e: e.activation(out=ab_, in_=ab_, func=AF.Ln, bias=C["one"][:, 0:1], scale=1.0), reads=["sm_ab", ("c", "one")], writes=["sm_ab"])
                S.op("dve", lambda e: e.scalar_tensor_tensor(out=dt_, in0=t_, scalar=0.0, in1=ab_, op0=ALU.max, op1=ALU.add), reads=["sm_t", "sm_ab"], writes=["sm_dt"])
                S.op("dve", lambda e: e.tensor_tensor(out=adt_, in0=dt_, in1=abc[:, g8], op=ALU.mult), reads=["sm_dt", ("b", "alog")], writes=["sm_adt"])
                S.op("pe", lambda e: e.matmul(P[4][:, 8:16], lhsT=C["U"][:], rhs=adt_, start=True, stop=True), reads=["sm_adt", ("c", "U")], writes=[("P", 4)])
                S.op("pe", lambda e: e.matmul(P[4][:, 16:24], lhsT=C["ones"][:], rhs=adt_, start=True, stop=True), reads=["sm_adt", ("c", "ones")], writes=[("P", 4)])
                S.op("dve", lambda e: e.tensor_copy(out=cst_, in_=P[4][:, 8:16]), reads=[("P", 4)], writes=["sm_cst"])
                S.op("dve", lambda e: e.tensor_tensor(out=dd_, in0=P[4][:, 16:24], in1=cst_, op=ALU.subtract), reads=[("P", 4), "sm_cst"], writes=["sm_dd"])
                S.op("act", lambda e: e.activation(out=dd_, in_=dd_, func=AF.Exp), reads=["sm_dd"], writes=["sm_dd"])
                S.op("dve", lambda e: e.tensor_tensor(out=wdt_, in0=dt_, in1=dd_, op=ALU.mult), reads=["sm_dt", "sm_dd"], writes=["sm_wdt"])
                S.op("act", lambda e: e.activation(out=etot_, in_=P[4][:, 16:24], func=AF.Exp), reads=[("P", 4)], writes=["sm_etot"])
                S.op("dve", lambda e: e.tensor_tensor(out=h3(xsd[:]), in0=h3(xbtm[:, 0:512]), in1=b3(wdt_), op=ALU.mult), reads=["xbtm", "sm_wdt"], writes=["xsd"])
                if full:
                  if K.sub & 1:
                    S.op("act", lambda e: e.activation(out=ecs_, in_=cst_, func=AF.Exp), reads=["sm_cst"], writes=["sm_ecs"])
                    S.op("dve", lambda e: e.tensor_tensor(out=h3(xs[:]), in0=h3(xbtm[:, 0:512]), in1=b3(dt_), op=ALU.mult), reads=["xbtm", "sm_dt"], writes=["xs"])
                  if K.sub & 2:
                    S.op("pe", lambda e, cs_=cs_: e.matmul(P[3][:, :], lhsT=xcT[:, 5, cs_], rhs=S_b[:, g, :], start=True, stop=True),
                         reads=[("xcT", 5), ("S_b", g)], writes=[("P", 3)])
                  if K.sub & 4:
                    S.op("pe", lambda e, cs_=cs_: e.matmul(P[4][:, 256:384], lhsT=xcT[:, 4, cs_], rhs=xcT[:, 5, cs_], start=True, stop=True),
                         reads=[("xcT", 4), ("xcT", 5)], writes=[("P", 4)])
                    S.op("act", lambda e: e.copy(out=cbt[:], in_=P[4][:, 256:384]), reads=[("P", 4)], writes=["cbt"])
                if full and (K.sub & 8):
                    nreal = int(os.environ.get("NREAL", "99"))
                    real = [
                        lambda: S.op("dve", lambda e: e.tensor_copy(out=Ff[:, 0:8], in_=cst_), reads=["sm_cst"], writes=["Ff"]),
                        lambda: S.op("dve", lambda e: e.tensor_copy(out=Hb[:], in_=Ff[:]), reads=["Ff"], writes=["Hb"]),
                        lambda: S.op("dve", lambda e: e.tensor_copy(out=Hf[:], in_=Hb[:]), reads=["Hb"], writes=["Hf"]),
                        lambda: S.op("dve", lambda e: e.scalar_tensor_tensor(out=Ff[:, 8:16], in0=Hf[:, 0:8], scalar=-1.0, in1=Ff[:, 0:8], op0=ALU.mult, op1=ALU.add), reads=["Ff", "Hf"], writes=["Ff"]),
                        lambda: S.op("dve", lambda e: e.tensor_copy(out=Hb[:], in_=Ff[:]), reads=["Ff"], writes=["Hb"]),
                        lambda: S.op("dve", lambda e: e.tensor_copy(out=Hf[:], in_=Hb[:]), reads=["Hb"], writes=["Hf"]),
                        lambda: S.op("dve", lambda e: e.scalar_tensor_tensor(out=Ff[:, 16:24], in0=Hf[:, 8:16], scalar=-1.0, in1=Ff[:, 8:16], op0=ALU.mult, op1=ALU.add), reads=["Ff", "Hf"], writes=["Ff"]),
                        lambda: S.op("dve", lambda e: e.tensor_copy(out=cs3tm[:], in_=Ff[:]), reads=["Ff"], writes=["cs3tm"]),
                    ]
                    for _f in real[:nreal]:
                        _f()
                    if int(os.environ.get("MSS", "9")) > 5:
                        S.op("pe", lambda e: e.transpose(out=pT[:, 0:128], in_=cs3tm[:, :], identity=C["ident"][:]), reads=["cs3tm", ("c", "ident")], writes=["pT"])
                    if int(os.environ.get("MSS", "9")) > 6:
                        S.op("act", lambda e: e.copy(out=cs3[:], in_=pT[:, 0:128]), reads=["pT"], writes=["cs3"])
                    if int(os.environ.get("MSS", "9")) > 7:
                        S.op("act", lambda e: e.mul(out=ncs3[:], in_=pT[:, 0:128], mul=-1.0), reads=["pT"], writes=["ncs3"])
                    for rnd in (range(2) if K.lvl >= 2 else ()):
                        for hl in range(4):
                            h = rnd * 4 + hl
                            o_ = P[5][:, hl * 128:(hl + 1) * 128]
                            S.op("pe", lambda e, o_=o_, h=h: e.matmul(o_, lhsT=C["sel"][:, h, :], rhs=cs3[:], start=True, stop=False), reads=["cs3", ("c", "sel")], writes=[("P", 5)])
                            S.op("pe", lambda e, o_=o_, h=h: e.matmul(o_, lhsT=ncs3[:], rhs=C["sel"][:, h, :], start=False, stop=False), reads=["ncs3", ("c", "sel")], writes=[("P", 5)])
                            S.op("pe", lambda e, o_=o_: e.matmul(o_, lhsT=C["ident"][:], rhs=C["negmask"][:], start=False, stop=True), reads=[("c", "ident"), ("c", "negmask")], writes=[("P", 5)])
                        S.op("act", lambda e, rnd=rnd: e.activation(out=Lt[:, rnd * 4:(rnd + 1) * 4, :], in_=P[5][:, :].rearrange("p (h l) -> p h l", h=4), func=AF.Exp),
                             reads=[("P", 5)], writes=["Lt"])
                    if K.lvl >= 2:
                        S.op("dve", lambda e: e.tensor_tensor(out=MT[:], in0=Lt[:], in1=cbt[:].unsqueeze(1).to_broadcast([128, 8, 128]), op=ALU.mult), reads=["Lt", "cbt"], writes=["MT"])
                    for h in (range(8) if K.lvl >= 2 else ()):
                        S.op("pe", lambda e, h=h: e.matmul(P[6][:, h * 64:(h + 1) * 64], lhsT=MT[:, h, :], rhs=xs[:, h * 64:(h + 1) * 64], start=True, stop=True),
                             reads=["MT", "xs"], writes=[("P", 6)])
                    if K.lvl >= 3:
                        S.op("dve", lambda e: e.tensor_tensor(out=h3(tmp[:]), in0=h3(P[3][:, :]), in1=b3(ecs_), op=ALU.mult), reads=[("P", 3), "sm_ecs"], writes=["tmp"])
                        S.op("dve", lambda e: e.tensor_tensor(out=yb[:], in0=P[6][:, :], in1=tmp[:], op=ALU.add), reads=[("P", 6), "tmp"], writes=["yb"])
                        S.op("pool", lambda e: e.tensor_tensor(out=h3(tmp2[:]), in0=h3(xbtm[:, 0:512]), in1=b3(dsk[:, g8]), op=ALU.mult), reads=["xbtm", ("b", "dsk")], writes=["tmp2"])
                        S.op("dve", lambda e: e.tensor_tensor(out=yb[:], in0=yb[:], in1=tmp2[:], op=ALU.add), reads=["yb", "tmp2"], writes=["yb"])
                S.op("pe", lambda e: e.matmul(P[3][:, :], lhsT=xbtm[:, 512:640], rhs=xsd[:], start=True, stop=True), reads=["xbtm", "xsd"], writes=[("P", 3)])
                S.op("dve", lambda e: e.tensor_tensor(out=h3(S_f[:, g, :]), in0=h3(S_f[:, g, :]), in1=b3(etot_), op=ALU.mult), reads=[("S_f", g), "sm_etot"], writes=[("S_f", g)])
                S.op("dve", lambda e: e.tensor_tensor(out=S_f[:, g, :], in0=S_f[:, g, :], in1=P[3][:, :], op=ALU.add), reads=[("S_f", g), ("P", 3)], writes=[("S_f", g)])
                S.op("act", lambda e: e.copy(out=S_b[:, g, :], in_=S_f[:, g, :]), reads=[("S_f", g)], writes=[("S_b", g)])
                if full and K.lvl >= 4:
                    for k in range(8):
                        S.op("pe", lambda e, k=k, gs_=gs_: e.matmul(P[2][:, :], lhsT=hT[:, k, gs_], rhs=wg_sb[:, k, 0:512], start=(k == 0), stop=(k == 7)),
                             reads=["wg_sb", ("hT", cc)], writes=[("P", 2)])
                    S.op("act", lambda e: e.activation(out=zs[:], in_=P[2][:, :], func=AF.Silu), reads=[("P", 2)], writes=["zs"])
                    S.op("dve", lambda e: e.tensor_tensor(out=yb[:], in0=yb[:], in1=zs[:], op=ALU.mult), reads=["yb", "zs"], writes=["yb"])
                    S.op("act", lambda e: e.activation(out=tmp[:], in_=yb[:], func=AF.Square, scale=1.0 / math.sqrt(512.0), accum_out=ssg[:]), reads=["yb"], writes=["tmp", "ssg"])
                    S.op("act", lambda e: e.activation(out=ssg[:], in_=ssg[:], func=AF.Sqrt, bias=C["eps"][:, 0:1], scale=1.0), reads=["ssg", ("c", "eps")], writes=["ssg"])
                    S.op("dve", lambda e: e.reciprocal(out=ssg[:], in_=ssg[:]), reads=["ssg"], writes=["ssg"])
                    S.op("dve", lambda e: e.scalar_tensor_tensor(out=yn[:], in0=yb[:], scalar=ssg[:, 0:1], in1=normw[:, g * 512:(g + 1) * 512], op0=ALU.mult, op1=ALU.mult),
                         reads=["yb", "ssg", ("b", "normw")], writes=["yn"])
                    S.op("dve", lambda e: e.memset(ssg[:], 0.0), reads=["ssg"], writes=["ssg"])
                    for kc in (range(4) if K.lvl >= 5 else ()):
                        S.op("pe", lambda e, kc=kc: e.transpose(out=pT[:, kc * 128:(kc + 1) * 128], in_=yn[:, kc * 128:(kc + 1) * 128], identity=C["ident"][:]),
                             reads=["yn", ("c", "ident")], writes=["pT"])
                    if K.lvl >= 5:
                        S.op("act", lambda e: e.copy(out=ynT[:], in_=pT[:, 0:512].rearrange("p (k t) -> p k t", k=4)), reads=["pT"], writes=["ynT"])
                    for half in (range(2) if K.lvl >= 5 else ()):
                        pb = 5 + half
                        for kc in range(4):
                            S.op("pe", lambda e, kc=kc, half=half, pb=pb: e.matmul(P[pb][:, :], lhsT=ynT[:, kc, :], rhs=wo_sb[:, kc, half * 512:(half + 1) * 512], start=(kc == 0), stop=(kc == 3)),
                                 reads=["ynT", "wo_sb"], writes=[("P", pb)])
                        S.op("dve", lambda e, half=half, pb=pb, cc=cc: e.tensor_tensor(out=x_sb[:, cc, half * 512:(half + 1) * 512], in0=x_sb[:, cc, half * 512:(half + 1) * 512], in1=P[pb][:, :], op=ALU.add),
                             reads=[("x", cc), ("P", pb)], writes=[("x", cc)])

        def ld(c):
            S.op("sp", lambda e, c=c: e.dma_start(out=xst[c % 2][:], in_=xp_dram[c * 128:(c + 1) * 128, :]), writes=[("xst", c % 2)], dma=True)

        def pre(c):
            if c == 0:
                ld(0)
                ld(1)
            elif c + 1 < NCH:
                ld(c + 1)
        norm_to_hT(K, C, lambda c: xst[c % 2][:], lambda c: ("xst", c % 2), gain, ("b", "gmix"), hT, "pre", pre=pre)
        for g in range(int(os.environ.get("PH1_G", "4"))):
            S.op("pool", lambda e, g=g: e.dma_start(out=wg_sb[:], in_=wg[g].rearrange("(kc p) c -> p kc c", p=128)), writes=["wg_sb"], dma=True)
            for tt in range(int(os.environ.get("PH1_T", "4"))):
                tile_pass(g, tt, False)
        allS = [("S_f", g) for g in range(4)]
        S.op("dve", lambda e: e.tensor_scalar(out=S_f[:].rearrange("p g n -> p (g n)"), in0=S_f[:].rearrange("p g n -> p (g n)"), scalar1=flag_sb[:, 0:1], scalar2=None, op0=ALU.mult),
             reads=allS + ["flag"], writes=allS)
        S.op("act", lambda e: e.copy(out=S_b[:], in_=S_f[:]), reads=allS, writes=[("S_b", g) for g in range(4)])
        S.op("dve", lambda e: e.tensor_scalar(out=hal[:].rearrange("p g j k -> p (g j k)"), in0=hal[:].rearrange("p g j k -> p (g j k)"), scalar1=flag_sb[:, 0:1], scalar2=None, op0=ALU.mult),
             reads=["hal", "flag"], writes=["hal"])
        if K.stage < 2:
            S.barrier()
            return
        norm_to_hT(K, C, lambda c: x_sb[:, c, :], lambda c: ("x", c), gain, ("b", "gmix"), hT, "mix")
        for g in range(int(os.environ.get("PH2_G", "4"))):
            S.op("pool", lambda e, g=g: e.dma_start(out=wg_sb[:], in_=wg[g].rearrange("(kc p) c -> p kc c", p=128)), writes=["wg_sb"], dma=True)
            S.op("pool", lambda e, g=g: e.dma_start(out=wo_sb[:], in_=wout[g * 512:(g + 1) * 512, :].rearrange("(kc p) c -> p kc c", p=128)), writes=["wo_sb"], dma=True)
            for tt in range(int(os.environ.get("PH2_T", "4"))):
                tile_pass(g, tt, True)
        S.barrier()


def build_A(stage=9):
    K = Ctx("A")
    K.stage = stage
    K.lvl = int(os.environ.get("MLVL", "9"))
    K.sub = int(os.environ.get("MSUB", "15"))
    S = K.S
    xo = K.din("xo", [TOK, D], F32)
    xp = K.din("xp", [TOK, D], F32)
    p0 = K.din("pl", [TOK, 256], F32)
    wgate = K.din("wgate", [D, FF], F32)
    wup = K.din("wup", [D, FF], F32)
    wdown = K.din("wdown", [FF, D], F32)
    wpp = K.din("wpp", [256, D], F32)
    wpg = K.din("wpg", [D, D], F32)
    out = K.dout("out", [TOK, D], F32)
    C = setup_common(K)
    x_sb = K.sb("x_sb", [128, NCH, D], F32)
    hT = K.sb("hT", [128, 8, TOK], BF16)
    for c in range(NCH):
        S.op("sp", lambda e, c=c: e.dma_start(out=x_sb[:, c, :], in_=xo[c * 128:(c + 1) * 128, :]), writes=[("x", c)], dma=True)
    if stage >= 1:
        mamba_mixer(K, C, x_sb, hT, xp)
    if stage >= 3:
        ffn_block(K, C, x_sb, hT, "gffn", wgate, wup, wdown)
    if stage >= 4:
        ple_block(K, C, x_sb, hT, p0, wpp, wpg)
    for c in range(NCH):
        S.op("sp", lambda e, c=c: e.dma_start(out=out[c * 128:(c + 1) * 128, :], in_=x_sb[:, c, :]), reads=[("x", c)], writes=[("out", c)], dma=True)
        K.outs.append(("out", c))
    return K.finish()


def host_consts():
    bf = ml_dtypes.bfloat16
    ident = np.eye(128, dtype=np.float32).astype(bf)
    U = np.triu(np.ones((128, 128), np.float32))
    ones = np.ones((128, 128), np.float32)
    sel = np.zeros((128, 8, 128), np.float32)
    for h in range(8):
        for m in range(3):
            sel[8 * m + h, h, :] = 1.0
    s_idx = np.arange(128)[:, None]
    l_idx = np.arange(128)[None, :]
    negmask = np.where(l_idx < s_idx, -30000.0, 0.0).astype(np.float32)
    return {"ident": ident, "U": U, "ones": ones, "sel": sel.astype(bf), "negmask": negmask.astype(bf)}


def prep_A(inp):
    x = np.asarray(inp["x"], np.float32)
    p = np.asarray(inp["p"], np.float32)
    w_in = np.asarray(inp["ssm_w_in"], np.float32)[0]
    wgs = []
    for g in range(4):
        cols = np.concatenate([np.arange(g * 512, (g + 1) * 512),
                               2048 + np.arange(g * 512, (g + 1) * 512),
                               4096 + np.arange(g * 128, (g + 1) * 128),
                               4608 + np.arange(g * 128, (g + 1) * 128),
                               5120 + np.arange(g * 8, (g + 1) * 8)])
        wgs.append(w_in[:, cols])
    wg = np.ascontiguousarray(np.stack(wgs))
    conv_w = np.asarray(inp["ssm_conv_w"], np.float32)[0]
    conv_b = np.asarray(inp["ssm_conv_b"], np.float32)[0]
    cw = np.zeros((128, 4, 6, 4), np.float32)
    cb = np.zeros((128, 4, 6), np.float32)
    for g in range(4):
        for j in range(6):
            if j < 4:
                ch = g * 512 + j * 128 + np.arange(128)
            elif j == 4:
                ch = 2048 + g * 128 + np.arange(128)
            else:
                ch = 2560 + g * 128 + np.arange(128)
            cw[:, g, j, :] = conv_w[:, ch].T
            cb[:, g, j] = conv_b[ch]
    shared = dict(host_consts())
    shared.update({
        "wg": wg, "cw": cw, "cb": cb,
        "gmix": np.ascontiguousarray(inp["norm_mix"][0:1]), "gffn": np.ascontiguousarray(inp["norm_ffn"][0:1]),
        "dtb": np.ascontiguousarray(inp["ssm_dt_bias"][0:1]), "alog": np.ascontiguousarray(inp["ssm_a_log"][0:1]),
        "dsk": np.ascontiguousarray(inp["ssm_d_skip"][0:1]), "normw": np.ascontiguousarray(inp["ssm_norm_w"][0:1]),
        "wout": np.ascontiguousarray(inp["ssm_w_out"][0]),
        "wgate": np.ascontiguousarray(inp["ffn_w_gate"][0]), "wup": np.ascontiguousarray(inp["ffn_w_up"][0]),
        "wdown": np.ascontiguousarray(inp["ffn_w_down"][0]),
        "wpp": np.ascontiguousarray(inp["ple_w_proj"][0]), "wpg": np.ascontiguousarray(inp["ple_w_gate"][0]),
    })
    shared = {k: np.asarray(v) for k, v in shared.items()}
    maps = []
    for c in range(8):
        b, half = c // 2, c % 2
        m = dict(shared)
        m["xo"] = np.ascontiguousarray(x[b, half * TOK:(half + 1) * TOK])
        m["xp"] = np.ascontiguousarray(x[b, 0:TOK])
        m["pl"] = np.ascontiguousarray(p[0, b, half * TOK:(half + 1) * TOK])
        m["flag"] = np.full((128, 1), float(half), np.float32)
        maps.append(m)
    return maps


_CACHE = {}


def run_A(inp):
    if "A" not in _CACHE:
        _CACHE["A"] = build_A()
    res = run_bass_kernel_spmd(_CACHE["A"], prep_A(inp), core_ids=list(range(8)))
    x1 = np.zeros((4, 4096, D), np.float32)
    for c in range(8):
        b, half = c // 2, c % 2
        x1[b, half * TOK:(half + 1) * TOK] = res.results[c]["out"]
    return x1


DILS = (1, 4, 16)
SLOPES = [2.0 ** (-8.0 * (h + 1) / 16.0) for h in range(16)]


def attention_mixer(K, C, x_sb, hT, xp_dram):
    S = K.S
    P = K.P
    pT = K.pT
    wq = K.din("wqkv", [24, D, 384], F32)
    wo = K.din("wo", [D, D], F32)
    DTd = K.din("DT", [128, 256], F32)
    nfd = K.din("nflag", [128, 1], F32)
    gqd = K.din("gq", [128, 3], F32)
    gkd = K.din("gk", [128, 1], F32)
    bod = K.din("bones", [128, 128], BF16)
    with contextlib.ExitStack() as es:
        gain = load_bcast(K, "gmix", D, es)
        hpT = K.sb("hpT", [128, 8, TOK], BF16, es)
        xst1 = K.sb("axst0", [128, 1024], F32, es)
        xst = [xst1, xst1]
        DT = K.sb("DTs", [128, 256], F32, es)
        DT0 = K.sb("DT0", [128, 256], F32, es)
        nf = K.sb("nfs", [128, 16], F32, es)
        gq = K.sb("gqs", [128, 16], F32, es)
        gk = K.sb("gks", [128, 16], F32, es)
        bones = K.sb("boness", [128, 128], BF16, es)
        w3 = K.sb("w3", [128, 8, 384], BF16, es)
        wo_sb = K.sb("awo", [128, 1024], BF16, es)
        Qn = K.sb("Qn", [128, TOK], BF16, es)
        Kn = K.sb("Kn", [128, TOK], BF16, es)
        Knp = K.sb("Knp", [128, TOK], BF16, es)
        Va = K.sb("Va", [128, 32, 2, 128], BF16, es)
        OT = [K.sb(f"OT{i}", [128, TOK], F32, es) for i in range(2)]
        sq = K.sb("asq", [128, 512], BF16, es)
        rs = K.sb("ars", [128, 512], F32, es)
        lg = rs
        PT = sq
        rd = rs
        att = K.sb("aatt", [128, 512], BF16, es)
        attb = sq
        S.op("sp", lambda e: e.dma_start(out=DT[:], in_=DTd), writes=["DT"], dma=True)
        S.op("sp", lambda e: e.dma_start(out=nf[:, 0:1], in_=nfd), writes=["nf"], dma=True)
        S.op("sp", lambda e: e.dma_start(out=gq[:, 0:3], in_=gqd), writes=["gq"], dma=True)
        S.op("sp", lambda e: e.dma_start(out=gk[:, 0:1], in_=gkd), writes=["gk"], dma=True)
        S.op("sp", lambda e: e.dma_start(out=bones[:], in_=bod), writes=["bones"], dma=True)
        S.op("dve", lambda e: e.tensor_copy(out=DT0[:], in_=DT[:]), reads=["DT"], writes=["DT0"])
        S.op("dve", lambda e: e.tensor_scalar(out=DT0[:, 0:128], in0=DT[:, 0:128], scalar1=nf[:, 0:1], scalar2=None, op0=ALU.add), reads=["DT", "nf"], writes=["DT0"])
        S.op("dve", lambda e: e.memset(Va[:], 1.0), writes=["Va"])

        def ld(c):
            S.op("sp", lambda e, c=c: e.dma_start(out=xst[c % 2][:], in_=xp_dram[c * 128:(c + 1) * 128, :]), writes=[("axst", 0)], dma=True)

        norm_to_hT(K, C, lambda c: xst[c % 2][:], lambda c: ("axst", 0), gain, ("b", "gmix"), hpT, "attp", pre=ld)
        for c in range(NCH):
            pass
        S.barrier()
        norm_to_hT(K, C, lambda c: x_sb[:, c, :], lambda c: ("x", c), gain, ("b", "gmix"), hT, "att")

        def proj_norm(src, n0, n1, wc, gcol, dst, dtok):
            for a in range(n0, n1, 512):
                w_ = min(512, n1 - a)
                for k in range(8):
                    S.op("pe", lambda e, k=k, a=a, w_=w_, wc=wc, src=src: e.matmul(P[0][:, 0:w_], lhsT=w3[:, k, wc:wc + 128], rhs=src[:, k, a:a + w_], start=(k == 0), stop=(k == 7)),
                         reads=["w3"] + [("hT", c) for c in range(NCH)], writes=[("P", 0)])
                S.op("act", lambda e, w_=w_: e.activation(out=sq[:, 0:w_], in_=P[0][:, 0:w_], func=AF.Square), reads=[("P", 0)], writes=["asq"])
                S.op("pe", lambda e, w_=w_: e.matmul(P[1][:, 0:w_], lhsT=bones[:], rhs=sq[:, 0:w_], start=True, stop=True), reads=["asq", "bones"], writes=[("P", 1)])
                S.op("act", lambda e, w_=w_: e.activation(out=rs[:, 0:w_], in_=P[1][:, 0:w_], func=AF.Sqrt, bias=C["eps"][:, 0:1], scale=1.0 / 64.0), reads=[("P", 1), ("c", "eps")], writes=["ars"])
                S.op("dve", lambda e, w_=w_: e.reciprocal(out=rs[:, 0:w_], in_=rs[:, 0:w_]), reads=["ars"], writes=["ars"])
                S.op("dve", lambda e, a=a, w_=w_, dst=dst, gcol=gcol: e.scalar_tensor_tensor(out=dst[:, a:a + w_], in0=P[0][:, 0:w_], scalar=gcol, in1=rs[:, 0:w_], op0=ALU.mult, op1=ALU.mult),
                     reads=[("P", 0), "ars", "gq", "gk"], writes=[dtok])

        for hp in range(8):
            for gi, d in enumerate(DILS):
                nb = 16 // d
                need = 128 * d
                S.op("pool", lambda e, gi=gi, hp=hp: e.dma_start(out=w3[:], in_=wq[gi * 8 + hp].rearrange("(kc p) c -> p kc c", p=128)), writes=["w3"], dma=True)
                proj_norm(hT, 0, TOK, 0, gq[:, gi:gi + 1], Qn, "Qn")
                proj_norm(hT, 0, TOK, 128, gk[:, 0:1], Kn, "Kn")
                proj_norm(hpT, TOK - need, TOK, 128, gk[:, 0:1], Knp, "Knp")
                vlist = []
                for r in range(d):
                    for b in range(nb):
                        vlist.append((hT, r + d * 128 * b, r * nb + b))
                    vlist.append((hpT, r + d * 128 * (nb - 1), 16 + r))
                for i0 in range(0, len(vlist), 4):
                    grp = vlist[i0:i0 + 4]
                    for si, (src, t0, blk) in enumerate(grp):
                        for k in range(8):
                            S.op("pe", lambda e, src=src, si=si, k=k, sl=slice(t0, t0 + 127 * d + 1, d): e.matmul(P[2][:, si * 128:(si + 1) * 128], lhsT=src[:, k, sl], rhs=w3[:, k, 256:384], start=(k == 0), stop=(k == 7)),
                                 reads=["w3"] + [("hT", c) for c in range(NCH)], writes=[("P", 2)])
                    for si, (src, t0, blk) in enumerate(grp):
                        S.op("act", lambda e, si=si, blk=blk: e.copy(out=Va[:, blk, :, 0:64], in_=P[2][:, si * 128:(si + 1) * 128].rearrange("p (h e) -> p h e", h=2)),
                             reads=[("P", 2)], writes=["Va"])
                for hd in range(2):
                    hh = hp * 2 + hd
                    cc_ = -8.0 * SLOPES[hh] * d
                    rows = slice(hd * 64, (hd + 1) * 64)
                    for r in range(d):
                        for qb in range(nb):
                            qcols = slice(r + d * 128 * qb, r + d * 128 * qb + 127 * d + 1, d)
                            if qb == 0:
                                ksrc, kc0, vprev, dt_ = Knp, r + d * 128 * (nb - 1), 16 + r, DT0
                            else:
                                ksrc, kc0, vprev, dt_ = Kn, r + d * 128 * (qb - 1), r * nb + qb - 1, DT
                            kcur0 = r + d * 128 * qb
                            vcur = r * nb + qb
                            S.op("pe", lambda e, ksrc=ksrc, rows=rows, ksl=slice(kc0, kc0 + 127 * d + 1, d), qcols=qcols: e.matmul(P[5][:, 0:128], lhsT=ksrc[rows, ksl], rhs=Qn[rows, qcols], start=True, stop=True),
                                 reads=["Kn", "Knp", "Qn"], writes=[("P", 5)])
                            S.op("pe", lambda e, rows=rows, ksl=slice(kcur0, kcur0 + 127 * d + 1, d), qcols=qcols: e.matmul(P[5][:, 128:256], lhsT=Kn[rows, ksl], rhs=Qn[rows, qcols], start=True, stop=True),
                                 reads=["Kn", "Qn"], writes=[("P", 5)])
                            S.op("dve", lambda e, dt_=dt_, cc_=cc_: e.scalar_tensor_tensor(out=lg[:, 0:256], in0=dt_[:], scalar=cc_, in1=P[5][:, 0:256], op0=ALU.mult, op1=ALU.add),
                                 reads=["DT", "DT0", ("P", 5)], writes=["ars"])
                            S.op("act", lambda e: e.activation(out=PT[:, 0:256], in_=lg[:, 0:256], func=AF.Exp, scale=0.125), reads=["ars"], writes=["asq"])
                            S.op("pe", lambda e, vprev=vprev, hd=hd: e.matmul(P[6][:, 0:128], lhsT=Va[:, vprev, hd, :], rhs=PT[:, 0:128], start=True, stop=False), reads=["Va", "asq"], writes=[("P", 6)])
                            S.op("pe", lambda e, vcur=vcur, hd=hd: e.matmul(P[6][:, 0:128], lhsT=Va[:, vcur, hd, :], rhs=PT[:, 128:256], start=False, stop=True), reads=["Va", "asq"], writes=[("P", 6)])
                            if gi == 0:
                                S.op("act", lambda e, qcols=qcols, hd=hd: e.copy(out=OT[hd][:, qcols], in_=P[6][:, 0:128]), reads=[("P", 6)], writes=[("OT", hd)])
                            else:
                                S.op("dve", lambda e, qcols=qcols, hd=hd: e.tensor_tensor(out=OT[hd][:, qcols], in0=OT[hd][:, qcols], in1=P[6][:, 0:128], op=ALU.add), reads=[("P", 6), ("OT", hd)], writes=[("OT", hd)])
            S.op("pool", lambda e, hp=hp: e.dma_start(out=wo_sb[:], in_=wo[hp * 128:(hp + 1) * 128, :]), writes=["awo"], dma=True)
            for a in range(0, TOK, 512):
                for hd in range(2):
                    S.op("dve", lambda e, a=a, hd=hd: e.reciprocal(out=rd[0:64, :], in_=OT[hd][64:128, a:a + 512]), reads=[("OT", hd)], writes=["ars"])
                    if hd == 0:
                        S.op("dve", lambda e, a=a, hd=hd: e.tensor_tensor(out=att[0:64, :], in0=OT[hd][0:64, a:a + 512], in1=rd[0:64, :], op=ALU.mult), reads=[("OT", hd), "ars"], writes=["aatt"])
                    else:
                        S.op("dve", lambda e, a=a, hd=hd: e.tensor_tensor(out=attb[0:64, :], in0=OT[hd][0:64, a:a + 512], in1=rd[0:64, :], op=ALU.mult), reads=[("OT", hd), "ars"], writes=["asq"])
                        S.op("act", lambda e: e.copy(out=att[64:128, :], in_=attb[0:64, :]), reads=["asq"], writes=["aatt"])
                for c4 in range(4):
                    c = a // 128 + c4
                    for half in range(2):
                        pb = 3 + half
                        S.op("pe", lambda e, c4=c4, half=half, pb=pb: e.matmul(P[pb][:, :], lhsT=att[:, c4 * 128:(c4 + 1) * 128], rhs=wo_sb[:, half * 512:(half + 1) * 512], start=True, stop=True),
                             reads=["aatt", "awo"], writes=[("P", pb)])
                        S.op("dve", lambda e, c=c, half=half, pb=pb: e.tensor_tensor(out=x_sb[:, c, half * 512:(half + 1) * 512], in0=x_sb[:, c, half * 512:(half + 1) * 512], in1=P[pb][:, :], op=ALU.add),
                             reads=[("x", c), ("P", pb)], writes=[("x", c)])
        S.barrier()


def build_B():
    K = Ctx("B")
    K.stage = 9
    S = K.S
    xo = K.din("xo", [TOK, D], F32)
    xp = K.din("xp", [TOK, D], F32)
    p1 = K.din("pl", [TOK, 256], F32)
    wgate = K.din("wgate", [D, FF], F32)
    wup = K.din("wup", [D, FF], F32)
    wdown = K.din("wdown", [FF, D], F32)
    wpp = K.din("wpp", [256, D], F32)
    wpg = K.din("wpg", [D, D], F32)
    out = K.dout("out", [TOK, D], F32)
    C = setup_common(K)
    x_sb = K.sb("x_sb", [128, NCH, D], F32)
    hT = K.sb("hT", [128, 8, TOK], BF16)
    for c in range(NCH):
        S.op("sp", lambda e, c=c: e.dma_start(out=x_sb[:, c, :], in_=xo[c * 128:(c + 1) * 128, :]), writes=[("x", c)], dma=True)
    attention_mixer(K, C, x_sb, hT, xp)
    ffn_block(K, C, x_sb, hT, "gffn", wgate, wup, wdown)
    ple_block(K, C, x_sb, hT, p1, wpp, wpg)
    for c in range(NCH):
        S.op("sp", lambda e, c=c: e.dma_start(out=out[c * 128:(c + 1) * 128, :], in_=x_sb[:, c, :]), reads=[("x", c)], writes=[("out", c)], dma=True)
        K.outs.append(("out", c))
    return K.finish()


def prep_B(inp, x1):
    p = np.asarray(inp["p"], np.float32)
    wqkv = np.asarray(inp["att_w_qkv"], np.float32)[0]
    blocks = []
    for gi in range(3):
        for hp in range(8):
            cols = np.concatenate([(gi * 3 + j) * 1024 + hp * 128 + np.arange(128) for j in range(3)])
            blocks.append(wqkv[:, cols])
    wq = np.ascontiguousarray(np.stack(blocks))
    qg = np.asarray(inp["att_q_norm"], np.float32)[0]
    kg = np.asarray(inp["att_k_norm"], np.float32)[0]
    gq = np.stack([np.tile(qg, 2)] * 3, axis=1).astype(np.float32)
    gk = np.tile(kg, 2)[:, None].astype(np.float32)
    j = np.arange(128)[:, None]
    i = np.arange(128)[None, :]
    prev = np.where(j >= i, 128 + i - j, 1e5)
    cur = np.where(j <= i, i - j, 1e5)
    DT = np.concatenate([prev, cur], axis=1).astype(np.float32)
    bones = np.kron(np.eye(2), np.ones((64, 64))).astype(ml_dtypes.bfloat16)
    shared = dict(host_consts())
    shared.update({
        "wqkv": wq, "wo": np.ascontiguousarray(inp["att_w_o"][0]), "DT": DT, "gq": gq, "gk": gk, "bones": bones,
        "gmix": np.ascontiguousarray(inp["norm_mix"][1:2]), "gffn": np.ascontiguousarray(inp["norm_ffn"][1:2]),
        "wgate": np.ascontiguousarray(inp["ffn_w_gate"][1]), "wup": np.ascontiguousarray(inp["ffn_w_up"][1]),
        "wdown": np.ascontiguousarray(inp["ffn_w_down"][1]),
        "wpp": np.ascontiguousarray(inp["ple_w_proj"][1]), "wpg": np.ascontiguousarray(inp["ple_w_gate"][1]),
    })
    shared = {k: np.asarray(v) for k, v in shared.items()}
    maps = []
    for c in range(8):
        b, half = c // 2, c % 2
        m = dict(shared)
        m["xo"] = np.ascontiguousarray(x1[b, half * TOK:(half + 1) * TOK])
        m["xp"] = np.ascontiguousarray(x1[b, 0:TOK])
        m["pl"] = np.ascontiguousarray(p[1, b, half * TOK:(half + 1) * TOK])
        m["nflag"] = np.full((128, 1), 0.0 if half == 1 else 1e5, np.float32)
        maps.append(m)
    return maps


def run_B(inp, x1):
    if "B" not in _CACHE:
        _CACHE["B"] = build_B()
    res = run_bass_kernel_spmd(_CACHE["B"], prep_B(inp, x1), core_ids=list(range(8)))
    y = np.zeros((4, 4096, D), np.float32)
    for c in range(8):
        b, half = c // 2, c % 2
        y[b, half * TOK:(half + 1) * TOK] = res.results[c]["out"]
    return y


def kernel(**inputs):
    x1 = run_A(inputs)
    return run_B(inputs, x1)
```
